# Optimizing a Trainium2 kernel written in Bass

```python
import jax, jax.numpy as jnp
from jax import lax
import numpy as np

D_MODEL = 2048
BATCH = 1
SEQ = 8192
DEPTH = 1

MEM_LEN = 256
D_MIX = 2 * D_MODEL
GM_WIDTH = D_MODEL // 2 * 2 // 2 * 2 // 2
GM_WIDTH = D_MIX // 2
GM_GROUPS = 4
GM_CHUNK = 128
SSM_WIDTH = D_MIX - GM_WIDTH
SSM_HEADDIM = 64
SSM_HEADS = SSM_WIDTH // SSM_HEADDIM
SSM_GROUPS = 8
SSM_STATE = 128
SSM_CONV = 4
SSM_CHUNK = 128
SSM_CONV_CH = SSM_WIDTH + 2 * SSM_GROUPS * SSM_STATE
IN_COLS = 2 * GM_WIDTH + SSM_WIDTH + SSM_CONV_CH + SSM_HEADS
XA_HEADS = 4
XA_HEADDIM = D_MODEL // XA_HEADS
D_FF = 5632
EPS = 1e-6

kernel_name = "hybrid_gmlp_ssd_macaron_memxattn"


def rmsnorm(x, w):
    xf = x.astype(jnp.float32)
    y = xf * lax.rsqrt(jnp.mean(xf * xf, axis=-1, keepdims=True) + EPS)
    return (y * w.astype(jnp.float32)).astype(x.dtype)


def swiglu(x, w_gu, w_down):
    g, u = jnp.split(x @ w_gu, 2, axis=-1)
    return (jax.nn.silu(g) * u) @ w_down


def chunked_sgu(u, v, v_norm_w, w_s, b_s):
    bn, s, _ = u.shape
    nc = s // GM_CHUNK
    cg = GM_WIDTH // GM_GROUPS
    v = rmsnorm(v, v_norm_w).reshape(bn, nc, GM_CHUNK, GM_GROUPS, cg)
    causal = jnp.tril(jnp.ones((GM_CHUNK, GM_CHUNK), dtype=bool))
    w = jnp.where(causal[None], w_s, jnp.zeros_like(w_s)).astype(v.dtype)
    mixed = jnp.einsum('gts,bcsgd->bctgd', w, v) + b_s.T.astype(v.dtype)[None, None, :, :, None]
    return u * mixed.reshape(bn, s, GM_WIDTH)


def causal_dwconv(x, w, b):
    k_w = w.shape[0]
    s = x.shape[1]
    xp = jnp.pad(x, ((0, 0), (k_w - 1, 0), (0, 0)))
    y = b + xp[:, 0:s] * w[0]
    for k in range(1, k_w):
        y = y + xp[:, k:k + s] * w[k]
    return y


def ssd_scan(x, dt, a, bm, cm):
    bn, s, h, p = x.shape
    g, n = bm.shape[-2:]
    r = h // g
    l = SSM_CHUNK
    nc = s // l
    x = x.reshape(bn, nc, l, g, r, p)
    dt = dt.reshape(bn, nc, l, g, r)
    bm = bm.reshape(bn, nc, l, g, n)
    cm = cm.reshape(bn, nc, l, g, n)
    cum = jnp.cumsum(dt * a.reshape(g, r), axis=2)
    xdt = x * dt[..., None]
    seg = cum[:, :, :, None] - cum[:, :, None]
    causal = jnp.tril(jnp.ones((l, l), dtype=bool))[:, :, None, None]
    decay = jnp.exp(jnp.where(causal, seg, -jnp.inf))
    scores = jnp.einsum('bclgn,bcsgn->bclsg', cm, bm)
    y_diag = jnp.einsum('bclsg,bclsgr,bcsgrp->bclgrp', scores, decay, xdt)
    decay_end = jnp.exp(cum[:, :, -1:] - cum)
    states = jnp.einsum('bclgn,bclgr,bclgrp->bcgrpn', bm, decay_end, xdt)
    chunk_decay = jnp.exp(cum[:, :, -1])

    def step(carry, inp):
        st, dec = inp
        return carry * dec[..., None, None] + st, carry

    init = jnp.zeros((bn, g, r, p, n), jnp.float32)
    _, prev = lax.scan(step, init, (jnp.moveaxis(states, 1, 0), jnp.moveaxis(chunk_decay, 1, 0)))
    prev = jnp.moveaxis(prev, 0, 1)
    y_off = jnp.einsum('bclgn,bcgrpn,bclgr->bclgrp', cm, prev, jnp.exp(cum))
    return (y_diag + y_off).reshape(bn, s, h, p)


def mamba2_group(z, xbc, dt_raw, conv_w, conv_b, dt_bias, a_log, d_skip, norm_w):
    f32 = jnp.float32
    xbc = jax.nn.silu(causal_dwconv(xbc, conv_w, conv_b))
    gn = SSM_GROUPS * SSM_STATE
    xs, bm, cm = jnp.split(xbc, [SSM_WIDTH, SSM_WIDTH + gn], axis=-1)
    bn, s, _ = xs.shape
    x_h = xs.reshape(bn, s, SSM_HEADS, SSM_HEADDIM).astype(f32)
    dt = jax.nn.softplus(dt_raw.astype(f32) + dt_bias.astype(f32))
    a = -jnp.exp(a_log.astype(f32))
    y = ssd_scan(x_h, dt, a,
                 bm.reshape(bn, s, SSM_GROUPS, SSM_STATE).astype(f32),
                 cm.reshape(bn, s, SSM_GROUPS, SSM_STATE).astype(f32))
    y = y + d_skip.astype(f32)[:, None] * x_h
    y = y.reshape(bn, s, SSM_WIDTH) * jax.nn.silu(z.astype(f32))
    return rmsnorm(y, norm_w).astype(z.dtype)


def memory_cross_attention(hn, memn, w_q, w_kv, w_o):
    bn, s, _ = hn.shape
    m = memn.shape[1]
    q = (hn @ w_q).reshape(bn, s, XA_HEADS, XA_HEADDIM)
    k, v = jnp.split(memn @ w_kv, 2, axis=-1)
    k = k.reshape(bn, m, XA_HEADS, XA_HEADDIM)
    v = v.reshape(bn, m, XA_HEADS, XA_HEADDIM)
    logits = jnp.einsum('bshd,bmhd->bhsm', q, k).astype(jnp.float32) * (XA_HEADDIM ** -0.5)
    probs = jax.nn.softmax(logits, axis=-1).astype(v.dtype)
    o = jnp.einsum('bhsm,bmhd->bshd', probs, v).reshape(bn, s, D_MODEL)
    return o @ w_o


def setup_inputs(seed: int = 0) -> dict:
    key = jax.random.key(seed)
    ks = jax.random.split(key, 32)
    f32 = jnp.float32

    def nrm(k, shape, scale):
        return jax.random.normal(k, shape, f32) * scale

    def gain(k, shape):
        return 1.0 + 0.1 * jax.random.normal(k, shape, f32)

    L = DEPTH
    dt0 = jnp.exp(jax.random.uniform(ks[12], (L, SSM_HEADS), f32) * (np.log(0.1) - np.log(0.001)) + np.log(0.001))
    dt_bias = dt0 + jnp.log(-jnp.expm1(-dt0))
    return {
        "x": nrm(ks[0], (BATCH, SEQ, D_MODEL), 1.0),
        "mem": nrm(ks[1], (BATCH, MEM_LEN, D_MODEL), 1.0),
        "ffn1_norm": gain(ks[2], (L, D_MODEL)),
        "ffn1_w_gu": nrm(ks[3], (L, D_MODEL, 2 * D_FF), D_MODEL ** -0.5),
        "ffn1_w_down": nrm(ks[4], (L, D_FF, D_MODEL), D_FF ** -0.5),
        "mix_norm": gain(ks[5], (L, D_MODEL)),
        "w_in": nrm(ks[6], (L, D_MODEL, IN_COLS), D_MODEL ** -0.5),
        "gm_v_norm": gain(ks[7], (L, GM_WIDTH)),
        "gm_w_s": nrm(ks[8], (L, GM_GROUPS, GM_CHUNK, GM_CHUNK), GM_CHUNK ** -0.5),
        "gm_b_s": gain(ks[9], (L, GM_GROUPS, GM_CHUNK)),
        "ssm_conv_w": nrm(ks[10], (L, SSM_CONV, SSM_CONV_CH), SSM_CONV ** -0.5),
        "ssm_conv_b": nrm(ks[11], (L, SSM_CONV_CH), 0.02),
        "ssm_dt_bias": dt_bias,
        "ssm_a_log": jnp.log(jax.random.uniform(ks[13], (L, SSM_HEADS), f32, 1.0, 16.0)),
        "ssm_d": gain(ks[14], (L, SSM_HEADS)),
        "ssm_norm": gain(ks[15], (L, SSM_WIDTH)),
        "w_out": nrm(ks[16], (L, D_MIX, D_MODEL), D_MIX ** -0.5),
        "xa_norm": gain(ks[17], (L, D_MODEL)),
        "mem_norm": gain(ks[18], (L, D_MODEL)),
        "xa_w_q": nrm(ks[19], (L, D_MODEL, D_MODEL), D_MODEL ** -0.5),
        "xa_w_kv": nrm(ks[20], (L, D_MODEL, 2 * D_MODEL), D_MODEL ** -0.5),
        "xa_w_o": nrm(ks[21], (L, D_MODEL, D_MODEL), D_MODEL ** -0.5),
        "ffn2_norm": gain(ks[22], (L, D_MODEL)),
        "ffn2_w_gu": nrm(ks[23], (L, D_MODEL, 2 * D_FF), D_MODEL ** -0.5),
        "ffn2_w_down": nrm(ks[24], (L, D_FF, D_MODEL), D_FF ** -0.5),
        "final_norm": gain(ks[25], (D_MODEL,)),
    }


def reference(x, mem, ffn1_norm, ffn1_w_gu, ffn1_w_down, mix_norm, w_in, gm_v_norm, gm_w_s, gm_b_s,
              ssm_conv_w, ssm_conv_b, ssm_dt_bias, ssm_a_log, ssm_d, ssm_norm, w_out,
              xa_norm, mem_norm, xa_w_q, xa_w_kv, xa_w_o, ffn2_norm, ffn2_w_gu, ffn2_w_down, final_norm):
    h = x
    split_at = [GM_WIDTH, 2 * GM_WIDTH, 2 * GM_WIDTH + SSM_WIDTH, 2 * GM_WIDTH + SSM_WIDTH + SSM_CONV_CH]
    for i in range(DEPTH):
        h = h + 0.5 * swiglu(rmsnorm(h, ffn1_norm[i]), ffn1_w_gu[i], ffn1_w_down[i])
        proj = rmsnorm(h, mix_norm[i]) @ w_in[i]
        gu, gv, z, xbc, dt_raw = jnp.split(proj, split_at, axis=-1)
        a_out = chunked_sgu(jax.nn.gelu(gu), jax.nn.gelu(gv), gm_v_norm[i], gm_w_s[i], gm_b_s[i])
        m_out = mamba2_group(z, xbc, dt_raw, ssm_conv_w[i], ssm_conv_b[i], ssm_dt_bias[i],
                             ssm_a_log[i], ssm_d[i], ssm_norm[i])
        h = h + jnp.concatenate([a_out, m_out], axis=-1) @ w_out[i]
        h = h + memory_cross_attention(rmsnorm(h, xa_norm[i]), rmsnorm(mem, mem_norm[i]),
                                       xa_w_q[i], xa_w_kv[i], xa_w_o[i])
        h = h + 0.5 * swiglu(rmsnorm(h, ffn2_norm[i]), ffn2_w_gu[i], ffn2_w_down[i])
    return rmsnorm(h, final_norm)
```

```python
import contextlib
import numpy as np
import concourse.bass as bass
import concourse.mybir as mybir
from concourse.bass_utils import run_bass_kernel_spmd

F32 = mybir.dt.float32
BF16 = mybir.dt.bfloat16
AF = mybir.ActivationFunctionType
ALU = mybir.AluOpType
AX = mybir.AxisListType

NCORES = 1
D = 2048
KC = 16
DFF = 5632
TT = 512
TH = 3
TW = TT + TH
NPASS = 16
EPS = 1e-6
INCOLS = 10272
NCONST = 896
PAYW = 2080
DEBUG = False
import os
STAGE = int(os.environ.get('KSTAGE', '8'))
NPASS_RUN = int(os.environ.get('KPASS', '16'))
KSUB = int(os.environ.get('KSUB', '0'))

ENGS = ("pe", "act", "dve", "pool", "sp")


class Lane:
    def __init__(self, sem, inc=16, total=False):
        self.sem, self.inc, self.total, self.count, self.last = sem, inc, total, 0, None


class Op:
    __slots__ = ("eng", "fn", "deps", "lane", "lane_val", "sig", "pos", "semval", "epoch")


class Builder:
    def __init__(self):
        self.ops = []
        self.lw = {}
        self.rd = {}
        self.last_on = {}

    enabled = True
    epoch = 0

    def op(self, eng, fn, reads=(), writes=(), lane=None, extra_deps=()):
        if not self.enabled:
            return -1
        i = len(self.ops)
        deps = set(extra_deps)
        writes = list(writes) + [r for r in reads if isinstance(r, tuple) and r[0] == "ps"]
        for r in reads:
            if r in self.lw:
                deps.add(self.lw[r])
        for w in writes:
            if w in self.lw:
                deps.add(self.lw[w])
            for j in self.rd.get(w, {}).values():
                deps.add(j)
        o = Op()
        o.eng, o.fn, o.deps, o.lane, o.sig = eng, fn, deps, lane, False
        o.epoch = self.epoch
        o.lane_val = 0
        if lane is not None:
            if lane.last is not None and not lane.total:
                deps.add(lane.last)
            lane.count += 1
            o.lane_val = lane.count * lane.inc
            lane.last = i
        key = eng if lane is None else ("lane", i)
        for r in reads:
            self.rd.setdefault(r, {})[key] = i
        for w in writes:
            self.lw[w] = i
            self.rd[w] = {}
        deps.discard(i)
        self.ops.append(o)
        self.last_on[eng] = i
        return i

    def barrier(self, engs=("pe", "act", "dve", "sp"), lanes=()):
        deps = set()
        for e in engs:
            if e in self.last_on:
                deps.add(self.last_on[e])
        for ln in lanes:
            if ln.last is not None:
                deps.add(ln.last)
        for e in engs:
            self.op(e, None, extra_deps=[d for d in deps])

    def emit(self, block, engsem, nc):
        ops = self.ops
        per = {e: [] for e in ENGS}
        for o in ops:
            o.pos = len(per[o.eng])
            per[o.eng].append(o)
        for o in ops:
            for d in o.deps:
                if ops[d].lane is None:
                    ops[d].sig = True
        for e in ENGS:
            c = 0
            cur_ep = -1
            for o in per[e]:
                if o.epoch != cur_ep:
                    cur_ep = o.epoch
                    c = 0
                o.semval = 0
                if o.lane is None and o.sig and o.fn is not None:
                    c += 1
                    o.semval = c
                elif o.lane is None and o.sig:
                    o.semval = c

        def run(e, ename):
            waited = {}
            for o in per[ename]:
                need = {}
                for d in o.deps:
                    do = ops[d]
                    if do.lane is not None:
                        sem = do.lane.sem
                        val = do.lane.count * do.lane.inc if do.lane.total else do.lane_val
                    else:
                        if do.fn is None:
                            continue
                        if do.eng == ename:
                            if ename in ("pe", "pool"):
                                continue
                            if KSUB & 8:
                                continue
                        sem = engsem[do.epoch][do.eng]
                        val = do.semval
                    if need.get(sem, (None, 0))[1] < val:
                        need[sem] = (sem, val)
                for sem, val in need.values():
                    if waited.get(sem, 0) < val:
                        e.wait_ge(sem, val)
                        waited[sem] = val
                if o.fn is None:
                    continue
                ins = o.fn(e)
                if o.lane is not None:
                    if o.lane.inc == 1:
                        ins.then_inc(o.lane.sem)
                    else:
                        ins.then_inc(o.lane.sem, o.lane.inc)
                elif o.sig:
                    ins.then_inc(engsem[o.epoch][ename], 1)

        @block.tensor
        def _(e):
            run(e, "pe")

        @block.scalar
        def _(e):
            run(e, "act")

        @block.vector
        def _(e):
            run(e, "dve")

        @block.gpsimd
        def _(e):
            run(e, "pool")

        @block.sync
        def _(e):
            run(e, "sp")


def build_program():
    nc = bass.Bass("TRN2", target_bir_lowering=False)

    def din(name, shape):
        return nc.dram_tensor(name, shape, F32, kind="ExternalInput").ap()

    x_in = din("x", [NPASS, TW, D])
    mem_in = din("mem", [256, D])
    consts_in = din("consts", [128, NCONST])
    bcw_in = din("bcw", [2, 128, D])
    ws_in = din("gm_w_s", [4, 128, 128])
    W = {}
    for nm, shp in (("ffn1_w_gu", [D, 2 * DFF]), ("ffn1_w_down", [DFF, D]), ("w_in", [D, INCOLS]),
                    ("w_out", [2 * D, D]), ("xa_w_q", [D, D]), ("xa_w_kv", [D, 2 * D]), ("xa_w_o", [D, D]),
                    ("ffn2_w_gu", [D, 2 * DFF]), ("ffn2_w_down", [DFF, D])):
        W[nm] = din(nm, shp)
    out_d = nc.dram_tensor("out", [NPASS, TT, D], F32, kind="ExternalOutput").ap()
    dbg_d = None
    if DEBUG:
        dbg_d = nc.dram_tensor("dbg", [128, KC * TW], F32, kind="ExternalOutput").ap()
    class _Dm:
        def ap(self):
            return self

        def __getitem__(self, k):
            return self

        def opt(self):
            return self
    cc_in = cc_out = [_Dm(), _Dm()]
    fdram = _Dm()
    if False:
        cc_in = [nc.dram_tensor(f"cc_in{e}", [128, PAYW], F32) for e in range(NPASS)]
        cc_out = [nc.dram_tensor(f"cc_out{e}", [NCORES * 128, PAYW], F32) for e in range(NPASS)]
        fdram = nc.dram_tensor("fdram", [128, D], F32)

    NBYTES = 211968
    big = nc.alloc_sbuf_tensor("big", [128, NBYTES // 4], F32)

    def carve(off, shape, dt=F32):
        n = 1
        for s_ in shape[1:]:
            n *= s_
        esz = 4 if dt == F32 else 2
        assert off % 4 == 0 and (n * esz) % 4 == 0, (off, shape)
        assert off + n * esz <= NBYTES, (off, shape)
        ap = big[:, off // 4: off // 4 + (n * esz) // 4]
        if dt != F32:
            ap = ap.bitcast(dt)
        if len(shape) == 3:
            ap = ap.rearrange("p (a b) -> p a b", a=shape[1])
        elif len(shape) == 4:
            ap = ap.rearrange("p (a b c) -> p a b c", a=shape[1], b=shape[2])
        return ap

    hT = carve(0, [128, KC, TW])
    nT = carve(32960, [128, KC, TW], BF16)
    mT = carve(32960, [128, KC, TT], BF16)
    Ftmp = carve(32960, [128, D])
    RING0 = 49440
    NSLOT = 5
    SLOT = 8192
    BC8K = 90400
    bcw = carve(BC8K, [128, D])
    C0 = 98592
    cst = carve(C0, [128, NCONST])
    o_ = C0 + NCONST * 4
    ident_f = carve(o_, [128, 128]); o_ += 512
    ones_f = carve(o_, [128, 128]); o_ += 512
    tri_le = carve(o_, [128, 128]); o_ += 512
    mgt = carve(o_, [128, 128]); o_ += 512
    wst = carve(o_, [128, 4, 128]); o_ += 2048
    ident_b = carve(o_, [128, 128], BF16); o_ += 256
    ones_b = carve(o_, [128, 128], BF16); o_ += 256
    abc = carve(o_, [128, 32]); o_ += 128
    S = o_
    assert S <= 107808, S
    S = 107808
    ncol = cst[:, 0:96].rearrange("p (a b) -> p a b", a=6)
    cw = cst[:, 96:224].rearrange("p (a b) -> p a b", a=32)
    cb = cst[:, 224:256]
    dbc = cst[:, 256:288]
    alog = cst[:, 288:320]
    dtb = cst[:, 320:352]
    rm = cst[:, 352:360]
    hflag = cst[:, 360:362]
    bsb = cst[:, 384:896].rearrange("p (a b) -> p a b", a=4)

    SQ = S + 86144
    RSN = S + 82048
    sq = carve(SQ, [128, KC, TT], BF16)
    rsn = carve(RSN, [128, TW])
    hid = carve(S, [128, 12, TW], BF16)
    sg = [carve(S + 12384 + i * 2080, [128, TW]) for i in range(2)]
    xin = [carve(S + 16640 + i * 8192, [128, D]) for i in range(2)]
    xh = carve(S + 33024, [128, D])
    ssqf = carve(S + 41216, [128, 8])
    XT = carve(S, [128, 4, D], BF16)
    GVT = XT
    SZ = carve(S + 16384, [128, 4, D], BF16)
    AT = carve(S + 16384, [128, KC, TT], BF16)
    BT = carve(S + 32768, [128, 8, TT], BF16)
    CT = carve(S + 40960, [128, 8, TT], BF16)
    BK = carve(S + 49152, [128, 4, 1024], BF16)
    PAY = carve(S + 57344, [128, PAYW])
    Y = carve(S + 65664, [128, D])
    MTK = carve(S + 73856, [128, D], BF16)
    GB = carve(S + 65664, [128, PAYW])
    RBF = carve(S + 77952, [128, D], BF16)
    DTS = S + 82048
    dtv = carve(DTS, [128, 4, 32])
    dAv = carve(DTS + 512, [128, 4, 32])
    cumv = carve(DTS + 1024, [128, 4, 32])
    totv = carve(DTS + 1536, [128, 4, 32])
    w1v = carve(DTS + 2048, [128, 4, 32])
    ecv = carve(DTS + 2560, [128, 4, 32])
    dcv = carve(DTS + 3072, [128, 4, 32])
    tmv = carve(DTS + 3584, [128, 4, 32])
    K0 = S + 86144
    gut = [carve(K0 + i * 2048, [128, TT]) for i in range(2)]
    ssqp = carve(K0 + 4096, [128, 4, 8])
    rv = carve(K0 + 4224, [128, 4])
    rv2 = carve(K0 + 4256, [128, 4])
    WSS = carve(K0 + 4352, [128, 4, 4, 128], BF16)
    tmpA = [carve(K0 + 8448 + i * 2048, [128, TT]) for i in range(2)]
    xp = [carve(K0 + i * 2080, [128, TW + 5]) for i in range(2)]
    acc = [carve(K0 + 4160 + i * 2048, [128, TT]) for i in range(2)]
    xc = [carve(K0 + 8256 + i * 1024, [128, TT], BF16) for i in range(2)]
    ltot = carve(K0 + 10304, [128, 32])
    scm = [carve(K0 + i * 512, [128, 128]) for i in range(2)]
    lseg = [carve(K0 + 1024 + i * 2048, [128, 4, 128]) for i in range(2)]
    Ebuf = [carve(K0 + 5120 + i * 2048, [128, 4, 128]) for i in range(2)]
    MTb = [carve(K0 + 9216 + i * 1024, [128, 4, 128], BF16) for i in range(2)]
    ytb = [carve(K0 + 11264 + i * 1024, [128, 256]) for i in range(2)]
    xdtg = [carve(K0 + 13312 + i * 512, [128, 256], BF16) for i in range(2)]
    xwg = [carve(K0 + 14336 + i * 512, [128, 256], BF16) for i in range(2)]
    ryv = carve(K0 + 15360, [128, 4])
    dmv = carve(K0 + 15376 + 16, [128, 32])
    djv = carve(K0 + 15520, [128, 32])
    memf = [carve(S + i * 8192, [128, D]) for i in range(2)]
    memT = carve(S + 16384, [128, KC, 256], BF16)
    kT = carve(S + 24576, [128, KC, 256], BF16)
    Vv = carve(S + 32768, [128, 2, D], BF16)
    qT = carve(S + 40960, [128, 4, TT], BF16)
    pT = carve(S + 45056, [128, 2, TT], BF16)
    rsx = carve(S + 47104, [128, TT])
    oT = carve(S + 49152, [128, 4, TT], BF16)
    mss = carve(S + 53248, [128, 4])

    psb = [nc.alloc_psum_tensor(f"psb{i}", [128, 512], F32)[:, :] for i in range(8)]

    stack = contextlib.ExitStack()
    with stack:
        engsems = [{e: stack.enter_context(nc.semaphore(f"es_{e}_{p}")) for e in ENGS} for p in range(NPASS_RUN + 1)]
        engsem = engsems
        lanes = {}

        def lane(name, inc=16, total=False):
            if name not in lanes:
                lanes[name] = Lane(stack.enter_context(nc.semaphore(f"ln_{name}")), inc, total)
            return lanes[name]

        B = Builder()

        def MM(out, lhsT, rhs, start, stop, reads, writes):
            return B.op("pe", lambda e: e.matmul(out, lhsT, rhs, start=start, stop=stop), reads, writes)

        def TR(out, in_, ident, reads, writes):
            return B.op("pe", lambda e: e.transpose(out, in_, ident), reads, writes)

        def ACT(out, in_, func, reads, writes, **kw):
            return B.op("act", lambda e: e.activation(out, in_, func, **kw), reads, writes)

        def TTs(out, in0, in1, op, reads, writes, eng="dve"):
            return B.op(eng, lambda e: e.tensor_tensor(out, in0, in1, op), reads, writes)

        def TS(out, in0, s1, s2, op0, op1, reads, writes, eng="dve", **kw):
            return B.op(eng, lambda e: e.tensor_scalar(out, in0, s1, s2, op0, op1, **kw), reads, writes)

        def TS1(out, in0, s1, op0, reads, writes, eng="dve"):
            return B.op(eng, lambda e: e.tensor_single_scalar(out, in0, s1, op0), reads, writes)

        def STT(out, in0, sc, in1, op0, op1, reads, writes, eng="dve", **kw):
            return B.op(eng, lambda e: e.scalar_tensor_tensor(out, in0, sc, in1, op0, op1, **kw), reads, writes)

        def CP(out, in_, reads, writes, eng="dve"):
            return B.op(eng, lambda e: e.tensor_copy(out, in_), reads, writes)

        def RCP(out, in_, reads, writes):
            return B.op("dve", lambda e: e.reciprocal(out, in_), reads, writes)

        def DMA(eng, out, in_, reads, writes, ln):
            return B.op(eng, lambda e: e.dma_start(out=out, in_=in_), reads, writes, lane=ln)

        pst = {"g": 0}

        def ps_main():
            b_ = pst["g"] % 8
            pst["g"] += 1
            return psb[b_], ("ps", b_)

        def ps_small(w):
            t, k = ps_main()
            return t[:, 0:w], k

        def ps_bf():
            t, k = ps_main()
            return t.bitcast(BF16)[:, 0:512], k

        def ps_tile(w):
            t, k = ps_main()
            return t[:, 0:w], k

        rst = {"i": 0}

        def wload(src, shape):
            i = rst["i"] % NSLOT
            rst["i"] += 1
            dst = carve(RING0 + i * SLOT, shape, BF16)
            DMA("pool", dst, src, [], [("w", i)], lane(f"w{i}"))
            return dst, ("w", i)

        def wcols(wap, r0, nr, c0, ncol_):
            return wap[r0:r0 + nr, :].rearrange("(k p) c -> p k c", p=128)[:, :, c0:c0 + ncol_]

        TILES = [(0, TT), (TT, TH)]

        cl = lane("const", total=True)
        DMA("sp", cst, consts_in, [], ["cst"], cl)
        wsraw = carve(S, [128, 4, 128])
        DMA("sp", wsraw, ws_in.rearrange("g t s -> t g s"), [], ["wsraw"], cl)
        B.enabled = not (KSUB & 16)
        B.op("dve", lambda e: e.memset(ident_f, 1.0), [], ["ident_f"])
        B.op("pool", lambda e: e.affine_select(out=ident_f, in_=ident_f, pattern=[[-1, 128]], compare_op=ALU.is_equal,
                                               fill=0.0, base=0, channel_multiplier=1), ["ident_f"], ["ident_f"])
        B.op("dve", lambda e: e.memset(ones_f, 1.0), [], ["ones_f"])
        B.op("dve", lambda e: e.memset(tri_le, 1.0), [], ["tri_le"])
        B.op("pool", lambda e: e.affine_select(out=tri_le, in_=tri_le, pattern=[[1, 128]], compare_op=ALU.is_ge,
                                               fill=0.0, base=0, channel_multiplier=-1), ["tri_le"], ["tri_le"])
        B.op("dve", lambda e: e.memset(mgt, 1.0), [], ["mgt"])
        B.op("pool", lambda e: e.affine_select(out=mgt, in_=mgt, pattern=[[-1, 128]], compare_op=ALU.is_gt,
                                               fill=0.0, base=0, channel_multiplier=1), ["mgt"], ["mgt"])
        CP(ident_b, ident_f, ["ident_f"], ["ident_b"])
        CP(ones_b, ones_f, ["ones_f"], ["ones_b"])
        ACT(abc, alog, AF.Exp, ["cst"], ["abc"])
        TS1(abc, abc, -1.0, ALU.mult, ["abc"], ["abc"])
        for g in range(4):
            pt, pk = ps_main()
            TR(pt[:, 0:128], wsraw[:, g, :], ident_f, ["wsraw", "ident_f"], [pk])
            TTs(wst[:, g, :], pt[:, 0:128], tri_le, ALU.mult, [pk, "tri_le"], ["wst"])
        B.enabled = True
        B.barrier()

        def rmsnorm_T(ci, tiles):
            for ti, (off, w) in enumerate(tiles):
                for kq in range(4):
                    ACT(sq[:, kq * 4:(kq + 1) * 4, 0:w], hT[:, kq * 4:(kq + 1) * 4, off:off + w], AF.Square,
                        [("h", k, ti) for k in range(kq * 4, kq * 4 + 4)], [("sq", kq)])
                pt, pk = ps_tile(w)
                for k in range(KC):
                    MM(pt[:, 0:w], ones_b, sq[:, k, 0:w], k == 0, k == KC - 1, [("sq", k // 4), "ones_b"], [pk])
                TS(rsn[:, 0:w], pt[:, 0:w], 1.0 / D, EPS, ALU.mult, ALU.add, [pk], ["rsn"])
                ACT(rsn[:, 0:w], rsn[:, 0:w], AF.Sqrt, ["rsn"], ["rsn"])
                RCP(rsn[:, 0:w], rsn[:, 0:w], ["rsn"], ["rsn"])
                for k in range(KC):
                    STT(nT[:, k, off:off + w], hT[:, k, off:off + w], ncol[:, ci, k:k + 1], rsn[:, 0:w],
                        ALU.mult, ALU.mult, [("h", k, ti), "rsn", "cst"], [("n", k, ti)])

        def ffn(wgu, wdn, ci, tiles):
            rmsnorm_T(ci, tiles)
            B.barrier()
            groups = [(0, 12), (12, 10), (22, 12), (34, 10)]
            for (j0, nj) in groups:
                for jp in range(0, nj, 2):
                    j = j0 + jp
                    gs, gk = wload(wcols(wgu, 0, D, j * 128, 256), [128, KC, 256])
                    us, uk = wload(wcols(wgu, 0, D, DFF + j * 128, 256), [128, KC, 256])
                    for jj in range(2):
                        pts = {}
                        for nm_, sl_, sk_ in (("g", gs, gk), ("u", us, uk)):
                            tl = [ps_tile(w) for (_, w) in tiles]
                            for k in range(KC):
                                for ti, (off, w) in enumerate(tiles):
                                    MM(tl[ti][0][:, 0:w], sl_[:, k, jj * 128:(jj + 1) * 128], nT[:, k, off:off + w],
                                       k == 0, k == KC - 1, [sk_, ("n", k, ti)], [tl[ti][1]])
                            pts[nm_] = tl
                        for ti, (off, w) in enumerate(tiles):
                            sgb = sg[ti]
                            ACT(sgb[:, 0:w], pts["g"][ti][0][:, 0:w], AF.Silu, [pts["g"][ti][1]], [("sg", ti)])
                            TTs(hid[:, jp + jj, off:off + w], sgb[:, 0:w], pts["u"][ti][0][:, 0:w], ALU.mult,
                                [("sg", ti), pts["u"][ti][1]], [("hid", jp + jj, ti)])
                for mp in range(8):
                    ds, dk = wload(wdn[j0 * 128:(j0 + nj) * 128, :].rearrange("(c p) n -> p c n", p=128)[:, :, mp * 256:(mp + 1) * 256],
                                   [128, nj, 256])
                    for mm_ in range(2):
                        m = mp * 2 + mm_
                        for ti, (off, w) in enumerate(tiles):
                            pt, pk = ps_tile(w)
                            for jj in range(nj):
                                MM(pt[:, 0:w], ds[:, jj, mm_ * 128:(mm_ + 1) * 128], hid[:, jj, off:off + w],
                                   jj == 0, jj == nj - 1, [dk, ("hid", jj, ti)], [pk])
                            STT(hT[:, m, off:off + w], pt[:, 0:w], 0.5, hT[:, m, off:off + w], ALU.mult, ALU.add,
                                [pk, ("h", m, ti)], [("h", m, ti)])
            B.barrier()

        def proj_res(wap, r0, nk, rhs_fn, rhs_keys, tiles):
            for mp in range(8):
                sl, sk = wload(wcols(wap, r0, nk * 128, mp * 256, 256), [128, nk, 256])
                for mm_ in range(2):
                    m = mp * 2 + mm_
                    for ti, (off, w) in enumerate(tiles):
                        pt, pk = ps_tile(w)
                        for k in range(nk):
                            MM(pt[:, 0:w], sl[:, k, mm_ * 128:(mm_ + 1) * 128], rhs_fn(k, off, w), k == 0, k == nk - 1,
                               [sk] + rhs_keys(k), [pk])
                        TTs(hT[:, m, off:off + w], pt[:, 0:w], hT[:, m, off:off + w], ALU.add,
                            [pk, ("h", m, ti)], [("h", m, ti)])

        xl = [lane("xin0"), lane("xin1")]
        ol = [lane("out0"), lane("out1")]

        def dump(ap3):
            if DEBUG:
                B.barrier()
                DMA("sp", dbg_d.rearrange("p (k t) -> p k t", k=KC), ap3, [], [], lane("dbg"))
                B.barrier(lanes=[lane("dbg")])

        for e_ in range(NPASS_RUN):
            B.epoch = e_ + 1
            B.enabled = not (KSUB & 2)
            for c in range(4):
                xb = xin[c % 2]
                DMA("sp", xb, x_in[e_, c * 128:(c + 1) * 128, :], [], [("xin", c % 2)], xl[c % 2])
                for kq in range(4):
                    pt, pk = ps_main()
                    for i in range(4):
                        k = kq * 4 + i
                        TR(pt[:, i * 128:(i + 1) * 128], xb[:, k * 128:(k + 1) * 128], ident_f,
                           [("xin", c % 2), "ident_f"], [pk])
                    dst = hT[:, kq * 4:(kq + 1) * 4, c * 128:(c + 1) * 128]
                    src = pt.rearrange("p (a b) -> p a b", a=4)
                    if kq % 2 == 0:
                        CP(dst, src, [pk], [("h", k_, 0) for k_ in range(kq * 4, kq * 4 + 4)])
                    else:
                        ACT(dst, src, AF.Copy, [pk], [("h", k_, 0) for k_ in range(kq * 4, kq * 4 + 4)])
            B.enabled = not (KSUB & 1)
            DMA("sp", xh[0:TH, :], x_in[e_, TT:TW, :], [], ["xh"], lane("xh"))
            pt, pk = ps_main()
            for k in range(KC):
                TR(pt[:, k * TH:(k + 1) * TH], xh[0:TH, k * 128:(k + 1) * 128], ident_f[0:TH, 0:TH], ["xh", "ident_f"], [pk])
            CP(hT[:, :, TT:TW], pt[:, 0:KC * TH].rearrange("p (a b) -> p a b", a=KC), [pk],
               [("h", k_, 1) for k_ in range(KC)])
            B.enabled = True
            B.barrier()

            B.enabled = STAGE >= 2
            ffn(W["ffn1_w_gu"], W["ffn1_w_down"], 0, TILES)

            B.enabled = STAGE >= 3
            rmsnorm_T(1, TILES)
            DMA("sp", bcw, bcw_in[0], [], ["bcw"], lane("bcw"))
            B.barrier()
            win = W["w_in"]
            MAIN = [(0, TT)]
            for ct in range(8):
                sl, sk = wload(wcols(win, 0, D, 2048 + ct * 256, 256), [128, KC, 256])
                for tc in range(4):
                    pt, pk = ps_main()
                    for k in range(KC):
                        MM(pt[:, 0:256], nT[:, k, tc * 128:(tc + 1) * 128], sl[:, k, :], k == 0, k == KC - 1,
                           [sk, ("n", k, 0)], [pk])
                    ACT(GVT[:, tc, ct * 256:(ct + 1) * 256], pt[:, 0:256], AF.Gelu_apprx_tanh, [pk], [("gvt", tc, ct)])
                    tb = tmpA[(ct * 4 + tc) % 2]
                    STT(tb[:, 0:256], GVT[:, tc, ct * 256:(ct + 1) * 256], 1.0, GVT[:, tc, ct * 256:(ct + 1) * 256],
                        ALU.mult, ALU.mult, [("gvt", tc, ct)], [("tmpA", (ct * 4 + tc) % 2), ("ssqp", tc, ct)],
                        accum_out=ssqp[:, tc, ct:ct + 1])
            B.op("dve", lambda e: e.tensor_reduce(out=rv, in_=ssqp, axis=AX.X, op=ALU.add),
                 [("ssqp", tc, ct) for tc in range(4) for ct in range(8)], ["rv"])
            TS(rv, rv, 1.0 / D, EPS, ALU.mult, ALU.add, ["rv"], ["rv"])
            ACT(rv, rv, AF.Sqrt, ["rv"], ["rv"])
            RCP(rv2, rv, ["rv"], ["rv2"])
            for tc in range(4):
                for g in range(4):
                    TS1(WSS[:, tc, g, :], wst[:, g, :], rv2[:, tc:tc + 1], ALU.mult,
                        ["wst", "rv2"], [("wss", tc, g)])
            for cp in range(8):
                sl, sk = wload(wcols(win, 0, D, cp * 256, 256), [128, KC, 256])
                for cc in range(2):
                    c = cp * 2 + cc
                    g = c // 4
                    pt, pk = ps_main()
                    for k in range(KC):
                        MM(pt, sl[:, k, cc * 128:(cc + 1) * 128], nT[:, k, 0:TT], k == 0, k == KC - 1,
                           [sk, ("n", k, 0)], [pk])
                    gb_ = gut[c % 2]
                    ACT(gb_, pt, AF.Gelu_apprx_tanh, [pk], [("gut", c % 2)])
                    p2, p2k = ps_main()
                    for tc in range(4):
                        MM(p2[:, tc * 128:(tc + 1) * 128], GVT[:, tc, c * 128:(c + 1) * 128], WSS[:, tc, g, :], True, True,
                           [("gvt", tc, c // 2), ("wss", tc, g)], [p2k])
                    ta = tmpA[c % 2]
                    STT(ta.rearrange("p (a b) -> p a b", a=4), p2.rearrange("p (a b) -> p a b", a=4),
                        ncol[:, 5, c:c + 1], bsb[:, g, :].unsqueeze(1).to_broadcast([128, 4, 128]),
                        ALU.mult, ALU.add, [p2k, "cst"], [("tmpA", c % 2)])
                    TTs(AT[:, c, :], ta, gb_, ALU.mult, [("tmpA", c % 2), ("gut", c % 2)], [("at", c)])
            proj_res(W["w_out"], 0, KC, lambda k, off, w: AT[:, k, off:off + w], lambda k: [("at", k)], MAIN)
            B.barrier()
            dump(hT)

            B.enabled = STAGE >= 4
            for cp in range(16):
                sl, sk = wload(wcols(win, 0, D, 6144 + cp * 256, 256), [128, KC, 256])
                for cc in range(2):
                    ch = cp * 2 + cc
                    pm, pmk = ps_main()
                    ph, phk = ps_small(TH)
                    for k in range(KC):
                        MM(pm, sl[:, k, cc * 128:(cc + 1) * 128], nT[:, k, 0:TT], k == 0, k == KC - 1, [sk, ("n", k, 0)], [pmk])
                        MM(ph, sl[:, k, cc * 128:(cc + 1) * 128], nT[:, k, TT:TW], k == 0, k == KC - 1, [sk, ("n", k, 1)], [phk])
                    xpb = xp[ch % 2]
                    ACT(xpb[:, TH:TW], pm, AF.Copy, [pmk], [("xp", ch % 2)])
                    ACT(xpb[:, 0:TH], ph, AF.Copy, [phk, "cst"], [("xp", ch % 2)], scale=(0.0 if e_ == 0 else 1.0))
                    ab = acc[ch % 2]
                    TS(ab, xpb[:, 3:3 + TT], cw[:, ch, 3:4], cb[:, ch:ch + 1], ALU.mult, ALU.add, [("xp", ch % 2), "cst"], [("acc", ch % 2)])
                    for kk in (2, 1, 0):
                        STT(ab, xpb[:, kk:kk + TT], cw[:, ch, kk:kk + 1], ab, ALU.mult, ALU.add,
                            [("xp", ch % 2), ("acc", ch % 2), "cst"], [("acc", ch % 2)])
                    if ch < 16:
                        xb_ = xc[ch % 2]
                        ACT(xb_, ab, AF.Silu, [("acc", ch % 2)], [("xc", ch % 2)])
                        pb_, pbk = ps_bf()
                        for tc in range(4):
                            TR(pb_[:, tc * 128:(tc + 1) * 128], xb_[:, tc * 128:(tc + 1) * 128], ident_b, [("xc", ch % 2), "ident_b"], [pbk])
                        CP(XT[:, :, ch * 128:(ch + 1) * 128], pb_.rearrange("p (a b) -> p a b", a=4), [pbk], [("xt", ch)])
                    elif ch < 24:
                        g = ch - 16
                        ACT(BT[:, g, :], ab, AF.Silu, [("acc", ch % 2)], [("bt", g)])
                        pb_, pbk = ps_bf()
                        for tc in range(4):
                            TR(pb_[:, tc * 128:(tc + 1) * 128], BT[:, g, tc * 128:(tc + 1) * 128], ident_b, [("bt", g), "ident_b"], [pbk])
                        CP(BK[:, :, g * 128:(g + 1) * 128], pb_.rearrange("p (a b) -> p a b", a=4), [pbk], [("bk", g)])
                    else:
                        g = ch - 24
                        ACT(CT[:, g, :], ab, AF.Silu, [("acc", ch % 2)], [("ct", g)])
            sl, sk = wload(wcols(win, 0, D, 10240, 32), [128, KC, 32])
            pd, pdk = ps_main()
            for tc in range(4):
                for k in range(KC):
                    MM(pd[:, tc * 32:(tc + 1) * 32], nT[:, k, tc * 128:(tc + 1) * 128], sl[:, k, :], k == 0, k == KC - 1,
                       [sk, ("n", k, 0)], [pdk])
            pd3 = pd[:, 0:128].rearrange("p (a b) -> p a b", a=4)
            bc4 = lambda v: v.unsqueeze(1).to_broadcast([128, 4, 32])
            TTs(tmv, pd3, bc4(dtb), ALU.add, [pdk, "cst"], ["tmv"])
            ACT(w1v, tmv, AF.Abs, ["tmv"], ["w1v"])
            ACT(w1v, w1v, AF.Exp, ["w1v"], ["w1v"], scale=-1.0)
            ACT(w1v, w1v, AF.Ln, ["w1v"], ["w1v"], bias=1.0)
            STT(dtv, tmv, 0.0, w1v, ALU.max, ALU.add, ["tmv", "w1v"], ["dtv"])
            TTs(dAv, dtv, bc4(abc), ALU.mult, ["dtv", "abc"], ["dAv"])
            pc, pck = ps_main()
            for tc in range(4):
                MM(pc[:, tc * 32:(tc + 1) * 32], tri_le, dAv[:, tc, :], True, True, ["tri_le", "dAv"], [pck])
                MM(pc[:, 128 + tc * 32:128 + (tc + 1) * 32], ones_f, dAv[:, tc, :], True, True, ["ones_f", "dAv"], [pck])
            CP(cumv, pc[:, 0:128].rearrange("p (a b) -> p a b", a=4), [pck], ["cumv"])
            CP(totv, pc[:, 128:256].rearrange("p (a b) -> p a b", a=4), [pck], ["totv"])
            TTs(tmv, totv, cumv, ALU.subtract, ["totv", "cumv"], ["tmv"])
            ACT(tmv, tmv, AF.Exp, ["tmv"], ["tmv"])
            TTs(w1v, dtv, tmv, ALU.mult, ["dtv", "tmv"], ["w1v"])
            ACT(ecv, cumv, AF.Exp, ["cumv"], ["ecv"])
            ACT(dcv, totv, AF.Exp, ["totv"], ["dcv"])
            TTs(ltot, totv[:, 0, :], totv[:, 1, :], ALU.add, ["totv"], ["ltot"])
            TTs(ltot, ltot, totv[:, 2, :], ALU.add, ["totv", "ltot"], ["ltot"])
            TTs(PAY[:, D:D + 32], ltot, totv[:, 3, :], ALU.add, ["totv", "ltot"], ["payd"])

            for ct in range(8):
                sl, sk = wload(wcols(win, 0, D, 4096 + ct * 256, 256), [128, KC, 256])
                for tc in range(4):
                    pt, pk = ps_main()
                    for k in range(KC):
                        MM(pt[:, 0:256], nT[:, k, tc * 128:(tc + 1) * 128], sl[:, k, :], k == 0, k == KC - 1,
                           [sk, ("n", k, 0)], [pk])
                    ACT(SZ[:, tc, ct * 256:(ct + 1) * 256], pt[:, 0:256], AF.Silu, [pk], [("sz", tc)])
            B.barrier()

            R = PAY[:, 0:D]
            bc64 = lambda v, n: v.unsqueeze(2).to_broadcast([128, n, 64])
            if e_ == 0:
                B.op("dve", lambda e: e.memset(R, 0.0), [], ["R"])
            B.barrier()
            B.enabled = STAGE >= 6
            for tc in range(4):
                ACT(RBF, R, AF.Copy, ["R"], ["rbf"])
                for g in range(8):
                    tok = slice(tc * 128, (tc + 1) * 128)
                    xd = xdtg[g % 2]
                    TTs(xd.rearrange("p (a b) -> p a b", a=4), XT[:, tc, g * 256:(g + 1) * 256].rearrange("p (a b) -> p a b", a=4),
                        bc64(dtv[:, tc, g * 4:(g + 1) * 4], 4), ALU.mult, [("xt", 2 * g), ("xt", 2 * g + 1), "dtv"], [("xdtg", g % 2)])
                    ps_, psk = ps_main()
                    MM(ps_[:, 0:128], BT[:, g, tok], CT[:, g, tok], True, True, [("bt", g), ("ct", g)], [psk])
                    sc = scm[g % 2]
                    TTs(sc, ps_[:, 0:128], tri_le, ALU.mult, [psk, "tri_le"], [("scm", g % 2)])
                    ls = lseg[g % 2]
                    TTs(ls, mgt.unsqueeze(1).to_broadcast([128, 4, 128]),
                        dAv[:, tc, g * 4:(g + 1) * 4].unsqueeze(2).to_broadcast([128, 4, 128]), ALU.mult,
                        ["mgt", "dAv"], [("lseg", g % 2)])
                    pe_, pek = ps_main()
                    for hh in range(4):
                        MM(pe_[:, hh * 128:(hh + 1) * 128], ls[:, hh, :], tri_le, True, True, [("lseg", g % 2), "tri_le"], [pek])
                    eb = Ebuf[g % 2]
                    ACT(eb, pe_.rearrange("p (a b) -> p a b", a=4), AF.Exp, [pek], [("eb", g % 2)])
                    mt_ = MTb[g % 2]
                    TTs(mt_, eb, sc.unsqueeze(1).to_broadcast([128, 4, 128]), ALU.mult, [("eb", g % 2), ("scm", g % 2)], [("mtb", g % 2)])
                    py, pyk = ps_main()
                    for hh in range(4):
                        MM(py[:, hh * 64:(hh + 1) * 64], mt_[:, hh, :], xd[:, hh * 64:(hh + 1) * 64], True, True,
                           [("mtb", g % 2), ("xdtg", g % 2)], [pyk])
                    MM(py[:, 256:512], CT[:, g, tok], RBF[:, g * 256:(g + 1) * 256], True, True, [("ct", g), "rbf"], [pyk])
                    yt = ytb[g % 2]
                    TTs(yt.rearrange("p (a b) -> p a b", a=4), py[:, 256:512].rearrange("p (a b) -> p a b", a=4),
                        bc64(ecv[:, tc, g * 4:(g + 1) * 4], 4), ALU.mult, [pyk, "ecv"], [("ytb", g % 2)])
                    TTs(Y[:, g * 256:(g + 1) * 256], py[:, 0:256], yt, ALU.add, [pyk, ("ytb", g % 2)], [("y", g)])
                    TTs(yt.rearrange("p (a b) -> p a b", a=4), XT[:, tc, g * 256:(g + 1) * 256].rearrange("p (a b) -> p a b", a=4),
                        bc64(dbc[:, g * 4:(g + 1) * 4], 4), ALU.mult, [("xt", 2 * g), ("xt", 2 * g + 1), "cst", ("y", g)], [("ytb", g % 2)])
                    TTs(Y[:, g * 256:(g + 1) * 256], Y[:, g * 256:(g + 1) * 256], yt, ALU.add, [("y", g), ("ytb", g % 2)], [("y", g)])
                    if True:
                        xw_ = xwg[g % 2]
                        TTs(xw_.rearrange("p (a b) -> p a b", a=4), XT[:, tc, g * 256:(g + 1) * 256].rearrange("p (a b) -> p a b", a=4),
                            bc64(w1v[:, tc, g * 4:(g + 1) * 4], 4), ALU.mult, [("xt", 2 * g), ("xt", 2 * g + 1), "w1v"], [("xwg", g % 2)])
                        pt, pk = ps_main()
                        MM(pt[:, 0:256], BK[:, tc, g * 128:(g + 1) * 128], xw_, True, True, [("bk", g), ("xwg", g % 2)], [pk])
                        rs_ = R[:, g * 256:(g + 1) * 256]
                        TTs(rs_.rearrange("p (a b) -> p a b", a=4), rs_.rearrange("p (a b) -> p a b", a=4),
                            bc64(dcv[:, tc, g * 4:(g + 1) * 4], 4), ALU.mult, ["R", "rbf", "dcv"], ["R"])
                        TTs(rs_, rs_, pt[:, 0:256], ALU.add, ["R", pk], ["R"])
                yk = [("y", g) for g in range(8)]
                TTs(Y, Y, SZ[:, tc, :], ALU.mult, yk + [("sz", tc)], yk)
                ACT(MTK, Y, AF.Square, yk, ["mtk", "ryv"], accum_out=ryv[:, 0:1])
                TS(ryv[:, 1:2], ryv[:, 0:1], 1.0 / D, EPS, ALU.mult, ALU.add, ["ryv"], ["ryv1"])
                ACT(ryv[:, 2:3], ryv[:, 1:2], AF.Sqrt, ["ryv1"], ["ryv2"])
                RCP(ryv[:, 3:4], ryv[:, 2:3], ["ryv2"], ["ryv3"])
                STT(MTK, Y, ryv[:, 3:4], bcw, ALU.mult, ALU.mult, yk + ["ryv3", "bcw", "mtk"], ["mtk"])
                for cq in range(4):
                    pb_, pbk = ps_bf()
                    for i in range(4):
                        c = cq * 4 + i
                        TR(pb_[:, i * 128:(i + 1) * 128], MTK[:, c * 128:(c + 1) * 128], ident_b, ["mtk", "ident_b"], [pbk])
                    dst = mT[:, cq * 4:(cq + 1) * 4, tc * 128:(tc + 1) * 128]
                    src = pb_.rearrange("p (a b) -> p a b", a=4)
                    if cq % 2 == 0:
                        ACT(dst, src, AF.Copy, [pbk], [("mt", cq)])
                    else:
                        CP(dst, src, [pbk], [("mt", cq)])
            B.barrier()
            proj_res(W["w_out"], D, KC, lambda k, off, w: mT[:, k, off:off + w], lambda k: [("mt", k // 4)], MAIN)
            B.barrier()

            B.enabled = STAGE >= 7
            rmsnorm_T(2, MAIN)
            for mc in range(2):
                DMA("sp", memf[mc], mem_in[mc * 128:(mc + 1) * 128, :], [], [("memf", mc)], lane("mem"))
            B.barrier()
            for mc in range(2):
                jk = carve(S + 65664, [128, D])
                ACT(jk, memf[mc], AF.Square, [("memf", mc)], ["jk", ("mss", mc)], accum_out=mss[:, mc:mc + 1])
                TS(mss[:, 2 + mc:3 + mc], mss[:, mc:mc + 1], 1.0 / D, EPS, ALU.mult, ALU.add, [("mss", mc)], [("mss2", mc)])
                ACT(mss[:, 2 + mc:3 + mc], mss[:, 2 + mc:3 + mc], AF.Sqrt, [("mss2", mc)], [("mss2", mc)])
                RCP(mss[:, mc:mc + 1], mss[:, 2 + mc:3 + mc], [("mss2", mc)], [("mss3", mc)])
                TS1(memf[mc], memf[mc], mss[:, mc:mc + 1], ALU.mult, [("memf", mc), ("mss3", mc)], [("memf", mc)])
                for kq in range(4):
                    pt, pk = ps_main()
                    for i in range(4):
                        k = kq * 4 + i
                        TR(pt[:, i * 128:(i + 1) * 128], memf[mc][:, k * 128:(k + 1) * 128], ident_f, [("memf", mc), "ident_f"], [pk])
                    for i in range(4):
                        k = kq * 4 + i
                        ACT(memT[:, k, mc * 128:(mc + 1) * 128], pt[:, i * 128:(i + 1) * 128], AF.Copy, [pk, "cst"], [("memT", k)],
                            scale=ncol[:, 4, k:k + 1])
            wkv = W["xa_w_kv"]
            for cp in range(8):
                sl, sk = wload(wcols(wkv, 0, D, cp * 256, 256), [128, KC, 256])
                for cc in range(2):
                    f = cp * 2 + cc
                    pt, pk = ps_main()
                    for k in range(KC):
                        MM(pt[:, 0:256], sl[:, k, cc * 128:(cc + 1) * 128], memT[:, k, :], k == 0, k == KC - 1, [sk, ("memT", k)], [pk])
                    ACT(kT[:, f, :], pt[:, 0:256], AF.Copy, [pk], [("kT", f)])
            for ct in range(8):
                sl, sk = wload(wcols(wkv, 0, D, D + ct * 256, 256), [128, KC, 256])
                for mc in range(2):
                    pt, pk = ps_main()
                    for k in range(KC):
                        MM(pt[:, 0:256], memT[:, k, mc * 128:(mc + 1) * 128], sl[:, k, :], k == 0, k == KC - 1, [sk, ("memT", k)], [pk])
                    CP(Vv[:, mc, ct * 256:(ct + 1) * 256], pt[:, 0:256], [pk], [("V", mc, ct)])
            for h in range(4):
                for cp in range(2):
                    sl, sk = wload(wcols(W["xa_w_q"], 0, D, h * 512 + cp * 256, 256), [128, KC, 256])
                    for cc in range(2):
                        f = cp * 2 + cc
                        pt, pk = ps_main()
                        for k in range(KC):
                            MM(pt, sl[:, k, cc * 128:(cc + 1) * 128], nT[:, k, 0:TT], k == 0, k == KC - 1, [sk, ("n", k, 0)], [pk])
                        ACT(qT[:, f, :], pt, AF.Copy, [pk], [("qT", f)], scale=float(512 ** -0.5))
                for mc in range(2):
                    pt, pk = ps_main()
                    for f in range(4):
                        MM(pt, kT[:, 4 * h + f, mc * 128:(mc + 1) * 128], qT[:, f, :], f == 0, f == 3, [("kT", 4 * h + f), ("qT", f)], [pk])
                    ACT(pT[:, mc, :], pt, AF.Exp, [pk], [("pT", mc)])
                pt, pk = ps_main()
                for mc in range(2):
                    MM(pt, ones_b, pT[:, mc, :], mc == 0, mc == 1, [("pT", mc), "ones_b"], [pk])
                RCP(rsx, pt, [pk], ["rsx"])
                for f in range(4):
                    pt, pk = ps_main()
                    for mc in range(2):
                        col = h * 512 + f * 128
                        MM(pt, Vv[:, mc, col:col + 128], pT[:, mc, :], mc == 0, mc == 1, [("V", mc, col // 256), ("pT", mc)], [pk])
                    TTs(oT[:, f, :], pt, rsx, ALU.mult, [pk, "rsx"], [("oT", f)])
                for hf in range(2):
                    sl, sk = wload(W["xa_w_o"][h * 512:(h + 1) * 512, :].rearrange("(k p) c -> p k c", p=128)[:, :, hf * 1024:(hf + 1) * 1024],
                                   [128, 4, 1024])
                    for mm_ in range(8):
                        m = hf * 8 + mm_
                        pt, pk = ps_main()
                        for f in range(4):
                            MM(pt, sl[:, f, mm_ * 128:(mm_ + 1) * 128], oT[:, f, :], f == 0, f == 3, [sk, ("oT", f)], [pk])
                        TTs(hT[:, m, 0:TT], pt, hT[:, m, 0:TT], ALU.add, [pk, ("h", m, 0)], [("h", m, 0)])
            B.barrier()

            B.enabled = STAGE >= 8
            ffn(W["ffn2_w_gu"], W["ffn2_w_down"], 3, MAIN)

            B.enabled = True
            DMA("sp", bcw, bcw_in[1], [], ["bcw"], lane("bcw"))
            B.barrier(lanes=[lane("bcw")])
            for c in range(4):
                xo = xin[c % 2]
                if KSUB & 4:
                    DMA("sp", out_d[e_, c * 128:(c + 1) * 128, :], xo, [("xin", c % 2)], [], ol[c % 2])
                    continue
                for kq in range(4):
                    pt, pk = ps_main()
                    for i in range(4):
                        k = kq * 4 + i
                        TR(pt[:, i * 128:(i + 1) * 128], hT[:, k, c * 128:(c + 1) * 128], ident_f, [("h", k, 0), "ident_f"], [pk])
                    CP(xo[:, kq * 512:(kq + 1) * 512], pt, [pk], [("xo", c % 2, kq)])
                    if not (KSUB & 64):
                        ACT(sg[0][:, 0:512], pt, AF.Square, [pk], ["sgj", ("ssqf", kq)], accum_out=ssqf[:, kq:kq + 1])
                xk = [("xo", c % 2, kq) for kq in range(4)]
                if not (KSUB & 128):
                    B.op("dve", lambda e: e.tensor_reduce(out=ssqf[:, 4:5], in_=ssqf[:, 0:4], axis=AX.X, op=ALU.add),
                         [("ssqf", kq) for kq in range(4)], ["ssqf4"])
                    TS(ssqf[:, 5:6], ssqf[:, 4:5], 1.0 / D, EPS, ALU.mult, ALU.add, ["ssqf4"], ["ssqf5"])
                    ACT(ssqf[:, 6:7], ssqf[:, 5:6], AF.Sqrt, ["ssqf5"], ["ssqf6"])
                    RCP(ssqf[:, 7:8], ssqf[:, 6:7], ["ssqf6"], ["ssqf7"])
                if not (KSUB & 256):
                    STT(xo, xo, ssqf[:, 7:8], bcw, ALU.mult, ALU.mult, xk + ["ssqf7", "bcw"], xk)
                DMA("sp", out_d[e_, c * 128:(c + 1) * 128, :], xo, xk, [], ol[c % 2])
            B.barrier(lanes=ol)

        block = stack.enter_context(nc.Block())
        B.emit(block, engsem, nc)
    return nc


_CACHE = {}


def _prep_inputs(inp):
    f32 = np.float32
    x = np.asarray(inp["x"], f32)[0]
    ins = {}
    shared = {}
    for nm in ("ffn1_w_gu", "ffn1_w_down", "w_in", "w_out", "xa_w_q", "xa_w_kv", "xa_w_o", "ffn2_w_gu", "ffn2_w_down"):
        shared[nm] = np.ascontiguousarray(np.asarray(inp[nm], f32)[0])
    shared["mem"] = np.ascontiguousarray(np.asarray(inp["mem"], f32)[0])
    shared["gm_w_s"] = np.ascontiguousarray(np.asarray(inp["gm_w_s"], f32)[0])
    bcw = np.empty((2, 128, D), f32)
    bcw[0] = np.asarray(inp["ssm_norm"], f32)[0][None, :]
    bcw[1] = np.asarray(inp["final_norm"], f32)[None, :]
    shared["bcw"] = bcw

    def col(v):
        return np.asarray(v, f32).reshape(KC, 128).T

    cbase = np.zeros((128, NCONST), f32)
    for i, nm in enumerate(("ffn1_norm", "mix_norm", "xa_norm", "ffn2_norm", "mem_norm", "gm_v_norm")):
        cbase[:, i * 16:(i + 1) * 16] = col(np.asarray(inp[nm], f32)[0])
    cwv = np.asarray(inp["ssm_conv_w"], f32)[0]
    cbase[:, 96:224] = cwv.reshape(4, 32, 128).transpose(2, 1, 0).reshape(128, 128)
    cbase[:, 224:256] = np.asarray(inp["ssm_conv_b"], f32)[0].reshape(32, 128).T
    cbase[:, 256:288] = np.asarray(inp["ssm_d"], f32)[0][None, :]
    cbase[:, 288:320] = np.asarray(inp["ssm_a_log"], f32)[0][None, :]
    cbase[:, 320:352] = np.asarray(inp["ssm_dt_bias"], f32)[0][None, :]
    cbase[:, 384:896] = np.asarray(inp["gm_b_s"], f32)[0].reshape(1, 512)
    in_maps = []
    for k in range(NCORES):
        xc = np.zeros((NPASS, TW, D), f32)
        for e in range(NPASS):
            t0 = 512 * e
            xc[e, 0:TT] = x[t0:t0 + TT]
            if t0 >= TH:
                xc[e, TT:TW] = x[t0 - TH:t0]
        c = cbase.copy()
        c[:, 360] = 0.0 if k == 0 else 1.0
        c[:, 361] = 1.0
        m = dict(shared)
        m["x"] = xc
        m["consts"] = c
        in_maps.append(m)
    return in_maps


def kernel(**inputs):
    if "nc" not in _CACHE:
        _CACHE["nc"] = build_program()
    nc = _CACHE["nc"]
    in_maps = _prep_inputs(inputs)
    res = run_bass_kernel_spmd(nc, in_maps, core_ids=list(range(NCORES)))
    out = np.zeros((1, 8192, D), np.float32)
    for k in range(NCORES):
        o = res.results[k]["out"]
        for e in range(NPASS_RUN):
            t0 = 512 * e
            out[0, t0:t0 + TT] = o[e]
    if DEBUG:
        _CACHE["dbg"] = [r["dbg"] for r in res.results]
    return out
```
